# Optimizing a Trainium2 kernel written in Bass

```python
import math
import jax, jax.numpy as jnp
from jax import lax
import numpy as np

D_MODEL = 4096
BATCH = 4
SEQ = 4096
DEPTH = 1

CTX_LEN = 256
GRID_W = 64
N_MOD = 9
D_FF = 11008
SSM_WIDTH = D_MODEL // 4
SSM_GROUP = 16
SSM_GROUPS = SSM_WIDTH // SSM_GROUP
SSM_STATE = 64
POOL_WINDOWS = (2, 4, 8, 16)
POOL_WIDTH = D_MODEL // 2
POOL_GROUP = POOL_WIDTH // len(POOL_WINDOWS)
D_IN = SSM_WIDTH + POOL_WIDTH + 2 * D_MODEL
RMS_EPS = 1e-6
LAMBDA_RE_MAX = -1e-4
STEP_MIN = 1e-3
STEP_MAX = 1e-1
HALF = 0.5

kernel_name = 'hybrid_s5_pool_macaron_prefix_dit'


def _rmsnorm(v, g):
    vf = v.astype(jnp.float32)
    vf = vf * lax.rsqrt(jnp.mean(vf * vf, axis=-1, keepdims=True) + RMS_EPS)
    return (vf * g.astype(jnp.float32)).astype(v.dtype)


def _pre(v, g, shift, scale):
    return _rmsnorm(v, g) * (1 + scale) + shift


def _swiglu(u, w_in, w_out):
    gate, up = jnp.split(u @ w_in, 2, axis=-1)
    return (jax.nn.silu(gate) * up) @ w_out


def _discretise(lam_re, lam_im, log_step, b_re, b_im):
    lam = lax.complex(jnp.minimum(lam_re.astype(jnp.float32), LAMBDA_RE_MAX), lam_im.astype(jnp.float32))
    step = jnp.exp(log_step.astype(jnp.float32))[..., None]
    lam_bar = jnp.exp(lam * step)
    b = lax.complex(b_re.astype(jnp.float32), b_im.astype(jnp.float32))
    b_bar = ((lam_bar - 1) / lam)[..., None] * b
    return lam_bar, b_bar


def _diag_scan(bu, lam_bar, reverse):
    a = jnp.broadcast_to(lam_bar, (bu.shape[0], 1) + lam_bar.shape)

    def combine(e1, e2):
        a1, b1 = e1
        a2, b2 = e2
        return a1 * a2, a2 * b1 + b2

    _, h = lax.associative_scan(combine, (a, bu), reverse=reverse, axis=0)
    return h


def _ssm_states(u_s, lam_bar, b_bar, h0):
    bsz, n, _ = u_s.shape
    ug = u_s.astype(jnp.float32).reshape(bsz, n, SSM_GROUPS, SSM_GROUP).astype(jnp.complex64)
    bu_f = jnp.einsum('blgk,gpk->lbgp', ug, b_bar[0])
    bu_b = jnp.einsum('blgk,gpk->lbgp', ug, b_bar[1])
    if h0 is not None:
        h0_f, h0_b = h0
        bu_f = bu_f.at[0].add(lam_bar[0] * h0_f)
        bu_b = bu_b.at[-1].add(lam_bar[1] * h0_b)
    h_f = _diag_scan(bu_f, lam_bar[0], reverse=False)
    h_b = _diag_scan(bu_b, lam_bar[1], reverse=True)
    return h_f, h_b


def _ssm_branch(u_s, h_f, h_b, c_re, c_im, d, w_glu, w_branch):
    bsz, n, _ = u_s.shape
    cm = lax.complex(c_re.astype(jnp.float32), c_im.astype(jnp.float32))
    y = jnp.real(jnp.einsum('lbgp,gkp->blgk', h_f, cm[0]) + jnp.einsum('lbgp,gkp->blgk', h_b, cm[1]))
    y = y.reshape(bsz, n, SSM_WIDTH).astype(u_s.dtype) + d * u_s
    z = jax.nn.gelu(y)
    a, b = jnp.split(z @ w_glu, 2, axis=-1)
    return (a * jax.nn.sigmoid(b)) @ w_branch


def _box_sum(v, w, axis):
    n = v.shape[axis]
    pad = [(0, 0)] * v.ndim
    pad[axis] = (1, 0)
    s = jnp.pad(jnp.cumsum(v, axis=axis), pad)
    idx = jnp.arange(n)
    lo = jnp.clip(idx - w // 2, 0, n)
    hi = jnp.clip(idx + (w - w // 2), 0, n)
    total = jnp.take(s, hi, axis=axis) - jnp.take(s, lo, axis=axis)
    return total, (hi - lo).astype(jnp.float32)


def _window_mean(v, w, on_grid):
    bsz, n, ch = v.shape
    vf = v.astype(jnp.float32)
    if on_grid:
        rows = n // GRID_W
        g = vf.reshape(bsz, rows, GRID_W, ch)
        s, cnt_r = _box_sum(g, w, 1)
        s, cnt_c = _box_sum(s, w, 2)
        mean = (s / (cnt_r[:, None] * cnt_c[None, :])[None, :, :, None]).reshape(bsz, n, ch)
    else:
        s, cnt = _box_sum(vf, w, 1)
        mean = s / cnt[None, :, None]
    return mean.astype(v.dtype)


def _pool_branch(u_p, on_grid, pool_w, pool_scale, w_branch):
    bsz, n, _ = u_p.shape
    groups = jnp.split(u_p, len(POOL_WINDOWS), axis=-1)
    mixed = jnp.stack([_window_mean(v, w, on_grid) - v for v, w in zip(groups, POOL_WINDOWS)], axis=2)
    y = jnp.einsum('blgc,gcd->blgd', mixed, pool_w).reshape(bsz, n, POOL_WIDTH)
    return (y * pool_scale) @ w_branch


def _token_mixer(u, lp, lam_bar, b_bar, h0, on_grid):
    splits = [SSM_WIDTH, SSM_WIDTH + POOL_WIDTH, SSM_WIDTH + POOL_WIDTH + D_MODEL]
    u_s, u_p, g_a, g_b = jnp.split(u @ lp['w_in'], splits, axis=-1)
    h_f, h_b = _ssm_states(u_s, lam_bar, b_bar, h0)
    y_a = _ssm_branch(u_s, h_f, h_b, lp['ssm_c_re'], lp['ssm_c_im'], lp['ssm_d'], lp['w_glu'], lp['w_branch_a'])
    y_b = _pool_branch(u_p, on_grid, lp['pool_w'], lp['pool_scale'], lp['w_branch_b'])
    merged = jax.nn.sigmoid(g_a) * y_a + jax.nn.sigmoid(g_b) * y_b
    return merged @ lp['w_out'], (h_f[-1], h_b[0])


def setup_inputs(seed: int = 0) -> dict:
    key = jax.random.key(seed)
    ks = jax.random.split(key, 32)
    f32 = jnp.float32

    def nrm(k, shape, scale):
        return jax.random.normal(k, shape, f32) * scale

    L, G, P, K = DEPTH, SSM_GROUPS, SSM_STATE, SSM_GROUP
    x = nrm(ks[0], (BATCH, SEQ, D_MODEL), 1.0)
    c = nrm(ks[1], (BATCH, D_MODEL), 1.0)
    ctx = nrm(ks[2], (BATCH, CTX_LEN, D_MODEL), 1.0)
    c_ctx = nrm(ks[3], (D_MODEL,), 1.0)
    w_mod = nrm(ks[4], (L, D_MODEL, N_MOD * D_MODEL), 0.5 * D_MODEL ** -0.5)
    b_mod = nrm(ks[5], (L, N_MOD * D_MODEL), 0.02)
    norm_g = 1.0 + nrm(ks[6], (L, 3, D_MODEL), 0.02)
    final_g = 1.0 + nrm(ks[7], (D_MODEL,), 0.02)
    ffn1_w_in = nrm(ks[8], (L, D_MODEL, 2 * D_FF), D_MODEL ** -0.5)
    ffn1_w_out = nrm(ks[9], (L, D_FF, D_MODEL), D_FF ** -0.5)
    ffn2_w_in = nrm(ks[10], (L, D_MODEL, 2 * D_FF), D_MODEL ** -0.5)
    ffn2_w_out = nrm(ks[11], (L, D_FF, D_MODEL), D_FF ** -0.5)
    w_in = nrm(ks[12], (L, D_MODEL, D_IN), D_MODEL ** -0.5)
    n_idx = jnp.arange(P, dtype=f32)
    ssm_lambda_re = -0.5 + nrm(ks[13], (L, 2, G, P), 0.01)
    ssm_lambda_im = math.pi * n_idx + nrm(ks[14], (L, 2, G, P), 0.01)
    ssm_log_step = jax.random.uniform(ks[15], (L, 2, G), f32, math.log(STEP_MIN), math.log(STEP_MAX))
    ssm_b_re = nrm(ks[16], (L, 2, G, P, K), (2 * K) ** -0.5)
    ssm_b_im = nrm(ks[17], (L, 2, G, P, K), (2 * K) ** -0.5)
    ssm_c_re = nrm(ks[18], (L, 2, G, K, P), (2 * P) ** -0.5)
    ssm_c_im = nrm(ks[19], (L, 2, G, K, P), (2 * P) ** -0.5)
    ssm_d = nrm(ks[20], (L, SSM_WIDTH), 1.0)
    w_glu = nrm(ks[21], (L, SSM_WIDTH, 2 * SSM_WIDTH), SSM_WIDTH ** -0.5)
    w_branch_a = nrm(ks[22], (L, SSM_WIDTH, D_MODEL), SSM_WIDTH ** -0.5)
    pool_w = nrm(ks[23], (L, len(POOL_WINDOWS), POOL_GROUP, POOL_GROUP), POOL_GROUP ** -0.5)
    pool_scale = 1.0 + nrm(ks[24], (L, POOL_WIDTH), 0.02)
    w_branch_b = nrm(ks[25], (L, POOL_WIDTH, D_MODEL), POOL_WIDTH ** -0.5)
    w_out = nrm(ks[26], (L, D_MODEL, D_MODEL), D_MODEL ** -0.5)
    return {'x': x, 'c': c, 'ctx': ctx, 'c_ctx': c_ctx, 'w_mod': w_mod, 'b_mod': b_mod,
            'norm_g': norm_g, 'final_g': final_g,
            'ffn1_w_in': ffn1_w_in, 'ffn1_w_out': ffn1_w_out, 'ffn2_w_in': ffn2_w_in, 'ffn2_w_out': ffn2_w_out,
            'w_in': w_in, 'ssm_lambda_re': ssm_lambda_re, 'ssm_lambda_im': ssm_lambda_im,
            'ssm_log_step': ssm_log_step, 'ssm_b_re': ssm_b_re, 'ssm_b_im': ssm_b_im,
            'ssm_c_re': ssm_c_re, 'ssm_c_im': ssm_c_im, 'ssm_d': ssm_d, 'w_glu': w_glu,
            'w_branch_a': w_branch_a, 'pool_w': pool_w, 'pool_scale': pool_scale,
            'w_branch_b': w_branch_b, 'w_out': w_out}


def reference(x, c, ctx, c_ctx, w_mod, b_mod, norm_g, final_g, ffn1_w_in, ffn1_w_out, ffn2_w_in, ffn2_w_out,
              w_in, ssm_lambda_re, ssm_lambda_im, ssm_log_step, ssm_b_re, ssm_b_im, ssm_c_re, ssm_c_im, ssm_d,
              w_glu, w_branch_a, pool_w, pool_scale, w_branch_b, w_out):
    bsz = x.shape[0]
    for l in range(DEPTH):
        last = l == DEPTH - 1
        mx = (jax.nn.silu(c) @ w_mod[l] + b_mod[l]).reshape(bsz, N_MOD, 1, D_MODEL)
        mx = [mx[:, i] for i in range(N_MOD)]
        mc = (jax.nn.silu(c_ctx) @ w_mod[l] + b_mod[l]).reshape(N_MOD, D_MODEL)
        lp = {'w_in': w_in[l], 'ssm_c_re': ssm_c_re[l], 'ssm_c_im': ssm_c_im[l], 'ssm_d': ssm_d[l],
              'w_glu': w_glu[l], 'w_branch_a': w_branch_a[l], 'pool_w': pool_w[l],
              'pool_scale': pool_scale[l], 'w_branch_b': w_branch_b[l], 'w_out': w_out[l]}
        lam_bar, b_bar = _discretise(ssm_lambda_re[l], ssm_lambda_im[l], ssm_log_step[l], ssm_b_re[l], ssm_b_im[l])

        x = x + HALF * mx[2] * _swiglu(_pre(x, norm_g[l, 0], mx[0], mx[1]), ffn1_w_in[l], ffn1_w_out[l])
        ctx = ctx + HALF * mc[2] * _swiglu(_pre(ctx, norm_g[l, 0], mc[0], mc[1]), ffn1_w_in[l], ffn1_w_out[l])

        uc = _pre(ctx, norm_g[l, 1], mc[3], mc[4])
        if last:
            h_f, h_b = _ssm_states(uc @ w_in[l][:, :SSM_WIDTH], lam_bar, b_bar, None)
            h_ctx = (h_f[-1], h_b[0])
        else:
            m_c, h_ctx = _token_mixer(uc, lp, lam_bar, b_bar, None, False)
            ctx = ctx + mc[5] * m_c
            ctx = ctx + HALF * mc[8] * _swiglu(_pre(ctx, norm_g[l, 2], mc[6], mc[7]), ffn2_w_in[l], ffn2_w_out[l])

        m_x, _ = _token_mixer(_pre(x, norm_g[l, 1], mx[3], mx[4]), lp, lam_bar, b_bar, h_ctx, True)
        x = x + mx[5] * m_x

        x = x + HALF * mx[8] * _swiglu(_pre(x, norm_g[l, 2], mx[6], mx[7]), ffn2_w_in[l], ffn2_w_out[l])
    return _rmsnorm(x, final_g)
```

```python
import types
import numpy as np
from contextlib import ExitStack
import concourse.bass as bass
import concourse.mybir as mybir
from concourse.bass_utils import run_bass_kernel_spmd

F32 = mybir.dt.float32
BF16 = mybir.dt.bfloat16
I32 = mybir.dt.int32
AF = mybir.ActivationFunctionType
ALU = mybir.AluOpType

D = 4096
KD = 32
FF = 11008
KF = 86
SSM_J = 4
NCTX = 256
PI = float(np.pi)


class Buf:
    __slots__ = ("name", "last_w", "readers")

    def __init__(self, name=""):
        self.name = name
        self.last_w = None
        self.readers = {}


def _snap(fn):
    if fn is None or fn.__closure__ is None:
        return fn
    cells = []
    for c in fn.__closure__:
        try:
            cells.append(types.CellType(c.cell_contents))
        except ValueError:
            cells.append(c)
    return types.FunctionType(fn.__code__, fn.__globals__, fn.__name__, fn.__defaults__, tuple(cells))


class Op:
    __slots__ = ("stream", "fn", "deps", "sig", "seq", "needed", "sigval", "waits", "ndma")

    def __init__(self, stream, fn, sig, ndma=0):
        self.stream = stream
        self.fn = _snap(fn)
        self.deps = []
        self.sig = sig
        self.seq = -1
        self.needed = False
        self.sigval = 0
        self.waits = []
        self.ndma = ndma


class Prog:
    ENGS = ("pe", "act", "dve", "pool", "sp")

    def __init__(self, nc, es):
        self.nc = nc
        self.es = es
        self.streams = {e: [] for e in self.ENGS}
        self.sigops = {}
        self.sems = {}
        for e in self.ENGS:
            self.sems[e] = es.enter_context(nc.semaphore("sem_" + e))
            self.sigops[e] = []
        self.ndsem = 0

    def dma_sem(self):
        self.ndsem += 1
        key = "dsem%d" % self.ndsem
        self.sems[key] = self.es.enter_context(self.nc.semaphore(key))
        self.sigops[key] = []
        return key

    def _record(self, op, reads, writes):
        deps = {}
        for b in reads:
            if b.last_w is not None:
                deps[id(b.last_w)] = b.last_w
        for b in writes:
            if b.last_w is not None:
                deps[id(b.last_w)] = b.last_w
            for r in b.readers.values():
                deps[id(r)] = r
        op.deps = list(deps.values())
        for b in writes:
            b.last_w = op
            b.readers = {}
        for b in reads:
            if b.last_w is not op:
                b.readers[op.sig] = op
        op.seq = len(self.sigops[op.sig])
        self.sigops[op.sig].append(op)
        self.streams[op.stream].append(op)
        return op

    def op(self, eng, fn, reads=(), writes=()):
        return self._record(Op(eng, fn, eng), reads, writes)

    def dma(self, queue, sem, fn, reads=(), writes=(), ndma=1):
        return self._record(Op(queue, fn, sem, ndma=ndma), reads, writes)

    def barrier(self):
        lasts = [ops[-1] for ops in self.sigops.values() if ops]
        for s in self.ENGS:
            op = Op(s, None, s)
            op.deps = [d for d in lasts]
            op.seq = len(self.sigops[s])
            self.sigops[s].append(op)
            self.streams[s].append(op)

    def emit(self):
        nc = self.nc
        for s in self.ENGS:
            waited = {}
            for op in self.streams[s]:
                best = {}
                for d in op.deps:
                    if d.fn is None:
                        continue
                    if d.sig == "pe" and op.stream == "pe" and op.sig == "pe":
                        continue
                    if waited.get(d.sig, -1) >= d.seq:
                        continue
                    if d.sig not in best or best[d.sig].seq < d.seq:
                        best[d.sig] = d
                for k, d in best.items():
                    d.needed = True
                    waited[k] = d.seq
                    op.waits.append(d)
        for k, ops in self.sigops.items():
            c = 0
            for op in ops:
                if op.ndma:
                    c += 16 * op.ndma
                    op.sigval = c
                elif op.needed:
                    c += 1
                    op.sigval = c
        engobj = {"pe": "tensor", "act": "scalar", "dve": "vector", "pool": "gpsimd", "sp": "sync"}
        with nc.Block() as block:
            for s in self.ENGS:
                ops = self.streams[s]
                sems = self.sems

                def body(eng, ops=ops, s=s):
                    for op in ops:
                        for d in op.waits:
                            eng.wait_ge(sems[d.sig], d.sigval)
                        if op.fn is None:
                            continue
                        r = op.fn(eng)
                        if op.ndma:
                            assert len(r) == op.ndma
                            for ins in r:
                                ins.then_inc(sems[op.sig], 16)
                        elif op.needed:
                            r.then_inc(sems[op.sig], 1)
                    if s == "sp":
                        for k, sops in self.sigops.items():
                            if sops and sops[-1].ndma:
                                eng.wait_ge(sems[k], sops[-1].sigval)

                getattr(block, engobj[s])(body)


def V(t, off, dims):
    fr = 1
    for s in t.shape[1:]:
        fr *= s
    return bass.AP(t, off, [[fr, 128]] + [list(d) for d in dims])


class Bank:
    def __init__(self, t):
        self.t = t
        self.buf = Buf()


def build(NLOC, NSEQ, dbg=None):
    nc = bass.Bass("TRN2", target_bir_lowering=False)
    ROWS = NSEQ // 64
    LT = NCTX + NSEQ
    NCH = (LT + 511) // 512

    def din(name, shape, dt=F32):
        return nc.dram_tensor(name, list(shape), dt, kind="ExternalInput").ap()

    x_d = din("x", [NSEQ, D])
    ctx_d = din("ctx", [NCTX, D])
    c2_d = din("c2", [128, 64])
    wmod_d = din("w_mod", [D, 9 * D])
    bmod_d = din("b_modT", [128, 288])
    ng_d = din("norm_gT", [128, 96])
    fg_d = din("final_g", [1, D])
    f1i_d = din("ffn1_w_in", [D, 2 * FF])
    f1o_d = din("ffn1_w_out", [FF, D])
    f2i_d = din("ffn2_w_in", [D, 2 * FF])
    f2o_d = din("ffn2_w_out", [FF, D])
    win_d = din("w_in", [D, 11264])
    wglu_d = din("w_glu", [1024, 2048])
    wba_d = din("w_branch_a", [1024, D])
    pw_d = din("pool_w", [2048, 512])
    wbb_d = din("w_branch_b", [2048, D])
    wout_d = din("w_out", [D, D])
    psc_d = din("pool_scaleT", [128, 16])
    sd_d = din("ssm_dT", [128, 8])
    sp_d = din("ssm_sp", [128, 2 * 3 * 32])
    bp_d = din("ssm_bp", [128, 2 * 5 * 512])
    cp_d = din("ssm_cp", [128, 2 * 2 * 512])
    bs_d = din("ssm_bs", [128, 2 * 2 * 512])
    bm_d = din("bmask", [128, 8])
    id_d = din("ident", [128, 128])
    io_d = din("iota", [128, 513])
    hs_d = din("hsel", [128, 2])
    out_d = nc.dram_tensor("out", [NLOC, D], F32, kind="ExternalOutput").ap()
    kw = {"kind": "ExternalOutput"} if dbg else {}
    x1_d = nc.dram_tensor("x1_scr", [NSEQ, D], F32, **kw).ap()
    us_d = nc.dram_tensor("us_scr", [1024, LT], F32, **kw).ap()
    up_d = nc.dram_tensor("up_scr", [2048, NSEQ], F32, **kw).ap()
    z_d = nc.dram_tensor("z_scr", [1024, NSEQ], BF16, **kw).ap()
    pm_d = nc.dram_tensor("pm_scr", [2048, NSEQ], BF16, **kw).ap()

    with ExitStack() as es:
        P = Prog(nc, es)

        def sbuf(stack, name, shape, dt):
            return stack.enter_context(nc.sbuf_tensor("s_" + name, list(shape), dt))

        banks = [Bank(es.enter_context(nc.psum_tensor("psb%d" % i, [128, 512], F32))) for i in range(8)]
        bstate = [0]

        def bank():
            b = banks[bstate[0] % 8]
            bstate[0] += 1
            return b

        ident = sbuf(es, "ident", [128, 128], F32)
        iota = sbuf(es, "iota", [128, 513], F32)
        MV = sbuf(es, "MV", [128, 2 * 3 * 3 * 32], F32)
        bmodT = sbuf(es, "bmodT", [128, 288], F32)
        ngT = sbuf(es, "ngT", [128, 96], F32)
        pscT = sbuf(es, "pscT", [128, 16], F32)
        sdT = sbuf(es, "sdT", [128, 8], F32)
        hsel = sbuf(es, "hsel", [128, 2], F32)
        cb = Buf("consts")
        mvb = Buf("mv")
        csem = P.dma_sem()

        def cload(t, src):
            P.dma("sp", csem, lambda e: [e.dma_start(out=t[:], in_=src)], writes=[cb])

        cload(ident, id_d)
        cload(iota, io_d)
        cload(bmodT, bmod_d)
        cload(ngT, ng_d)
        cload(pscT, psc_d)
        cload(sdT, sd_d)
        cload(hsel, hs_d)

        def mv(j, kind, i):
            o = ((j * 3 + kind) * 3 + i) * 32
            return MV[:, o:o + 32]

        NWF, NWB = 3, 2
        R = {}

        def make_rings(stack, pfx, nwf=NWF, nwb=NWB):
            R["nwf"], R["nwb"] = nwf, nwb
            R["wf"] = [sbuf(stack, pfx + "wf%d" % i, [128, 8, 256], F32) for i in range(nwf)]
            R["wfb"] = [Buf() for _ in range(nwf)]
            R["wfs"] = [P.dma_sem() for _ in range(nwf)]
            R["wb"] = [sbuf(stack, pfx + "wb%d" % i, [128, 8, 256], BF16) for i in range(nwb)]
            R["wbb"] = [Buf() for _ in range(nwb)]
        wst = [0, 0]

        def stage(W, r0, nk, c0, ncols):
            i = wst[0] % R["nwf"]
            wst[0] += 1
            j = wst[1] % R["nwb"]
            wst[1] += 1
            wf, wb = R["wf"], R["wb"]
            src = W[r0:r0 + nk * 128, c0:c0 + ncols].rearrange("(k p) c -> p k c", p=128)
            P.dma("sp", R["wfs"][i], lambda e: [e.dma_start(out=wf[i][:, :nk, :ncols], in_=src)], writes=[R["wfb"][i]])
            P.op("act", lambda e: e.copy(out=wb[j][:, :nk, :ncols], in_=wf[i][:, :nk, :ncols]),
                 reads=[R["wfb"][i]], writes=[R["wbb"][j]])
            return wb[j], R["wbb"][j]

        with ExitStack() as p0:
            make_rings(p0, "p0")
            c2s = sbuf(p0, "c2s", [128, 64], F32)
            s2b = sbuf(p0, "s2b", [128, 64], BF16)
            modraw = sbuf(p0, "modraw", [128, 576], F32)
            c2b, s2bb, mrb = Buf(), Buf(), Buf()
            P.dma("sp", csem, lambda e: [e.dma_start(out=c2s[:], in_=c2_d)], writes=[c2b])
            P.op("act", lambda e: e.activation(out=s2b[:], in_=c2s[:], func=AF.Silu), reads=[c2b], writes=[s2bb])
            pA, pB = bank(), bank()
            P.op("dve", lambda e: e.memset(pA.t[:], 0.0), writes=[pA.buf])
            P.op("dve", lambda e: e.memset(pB.t[:], 0.0), writes=[pB.buf])
            for cp in range(144):
                for kg in range(4):
                    wt, wbuf = stage(wmod_d, kg * 1024, 8, cp * 256, 256)
                    for kk in range(8):
                        for cc in range(2):
                            j = cp * 2 + cc
                            bk = pA if j < 256 else pB
                            off = (j % 256) * 2
                            k = kg * 8 + kk
                            P.op("pe", lambda e, bk=bk, off=off, wt=wt, kk=kk, cc=cc, k=k: e.matmul(
                                bk.t[:, off:off + 2], lhsT=wt[:, kk, cc * 128:(cc + 1) * 128],
                                rhs=s2b[:, 2 * k:2 * k + 2], start=False, stop=(k == 31)),
                                reads=[wbuf, s2bb], writes=[bk.buf])
            P.op("dve", lambda e: e.tensor_tensor(out=V(modraw, 0, [[2, 256], [1, 2]]), in0=V(pA.t, 0, [[2, 256], [1, 2]]),
                                                  in1=V(bmodT, 0, [[1, 256], [0, 2]]), op=ALU.add),
                 reads=[pA.buf, cb], writes=[mrb])
            P.op("dve", lambda e: e.tensor_tensor(out=V(modraw, 512, [[2, 32], [1, 2]]), in0=V(pB.t, 0, [[2, 32], [1, 2]]),
                                                  in1=V(bmodT, 256, [[1, 32], [0, 2]]), op=ALU.add),
                 reads=[pB.buf, cb], writes=[mrb])
            coefs = [0.5, 1.0, 0.5]
            for j in range(2):
                for i in range(3):
                    sh = V(modraw, (3 * i) * 64 + j, [[2, 32]])
                    sc = V(modraw, (3 * i + 1) * 64 + j, [[2, 32]])
                    gt = V(modraw, (3 * i + 2) * 64 + j, [[2, 32]])
                    P.op("dve", lambda e, sc=sc, i=i, j=j: e.scalar_tensor_tensor(
                        out=mv(j, 0, i), in0=sc, scalar=1.0, in1=ngT[:, i * 32:(i + 1) * 32], op0=ALU.add, op1=ALU.mult),
                        reads=[mrb, cb], writes=[mvb])
                    P.op("dve", lambda e, sh=sh, i=i, j=j: e.tensor_copy(out=mv(j, 1, i), in_=sh), reads=[mrb], writes=[mvb])
                    P.op("dve", lambda e, gt=gt, i=i, j=j: e.tensor_scalar(
                        out=mv(j, 2, i), in0=gt, scalar1=coefs[i], scalar2=None, op0=ALU.mult), reads=[mrb], writes=[mvb])
            P.barrier()

        def tile_phase(stack, pfx):
            S = {}
            make_rings(stack, pfx)
            S["xt"] = sbuf(stack, pfx + "xt", [128, 4, D], F32)
            S["xn"] = sbuf(stack, pfx + "xn", [128, 32, 512], BF16)
            S["work"] = sbuf(stack, pfx + "work", [128, 28672], BF16)
            S["sg"] = [sbuf(stack, pfx + "sg%d" % i, [128, 512], F32) for i in range(2)]
            S["o32"] = [sbuf(stack, pfx + "o32_%d" % i, [128, 512], F32) for i in range(3)]
            S["dg"] = sbuf(stack, pfx + "dg", [128, 4, 128], F32)
            S["st"] = sbuf(stack, pfx + "st", [128, 8], F32)
            S["xtb"] = [Buf() for _ in range(4)]
            S["xts"] = [P.dma_sem() for _ in range(4)]
            S["xos"] = [P.dma_sem() for _ in range(4)]
            S["xnb"] = [Buf() for _ in range(32)]
            S["wkb"] = [Buf() for _ in range(7)]
            S["sgb"] = [Buf() for _ in range(2)]
            S["o32b"] = [Buf() for _ in range(3)]
            S["o32s"] = [P.dma_sem() for _ in range(3)]
            S["dgb"] = [Buf() for _ in range(4)]
            S["stb"] = [Buf() for _ in range(4)]
            S["rr"] = [0, 0]
            return S

        def load_x(S, src, ng):
            xt = S["xt"]
            for g in range(ng):
                P.dma("sp", S["xts"][g], lambda e, g=g: [e.dma_start(out=xt[:, g, :], in_=src[g * 128:(g + 1) * 128, :])],
                      writes=[S["xtb"][g]])

        def store_x(S, dst, ng, wbufs=()):
            xt = S["xt"]
            for g in range(ng):
                P.dma("sp", S["xos"][g], lambda e, g=g: [e.dma_start(out=dst[g * 128:(g + 1) * 128, :], in_=xt[:, g, :])],
                      reads=[S["xtb"][g]], writes=list(wbufs))

        def rstd_of(S, g, junk_out, junk_bufs):
            xt, st = S["xt"], S["st"]
            P.op("act", lambda e: e.activation(out=junk_out, in_=V(xt, g * D, [[128, 32], [1, 128]]), func=AF.Square,
                                               accum_out=st[:, g:g + 1]),
                 reads=[S["xtb"][g]], writes=junk_bufs + [S["stb"][g]])
            P.op("dve", lambda e: e.tensor_scalar(out=st[:, g:g + 1], in0=st[:, g:g + 1], scalar1=1.0 / D, scalar2=1e-6,
                                                  op0=ALU.mult, op1=ALU.add), reads=[S["stb"][g]], writes=[S["stb"][g]])
            P.op("act", lambda e: e.activation(out=st[:, g:g + 1], in_=st[:, g:g + 1], func=AF.Sqrt),
                 reads=[S["stb"][g]], writes=[S["stb"][g]])
            P.op("dve", lambda e: e.reciprocal(out=st[:, g:g + 1], in_=st[:, g:g + 1]), reads=[S["stb"][g]], writes=[S["stb"][g]])

        def norm_to_xn(S, ng, A, B):
            xt, xn, dg, st = S["xt"], S["xn"], S["dg"], S["st"]
            for g in range(ng):
                rstd_of(S, g, V(xn, g * 128, [[512, 32], [1, 128]]), list(S["xnb"]))
                P.op("dve", lambda e, g=g: e.tensor_scalar(out=dg[:, g, :], in0=ident[:], scalar1=st[:, g:g + 1], scalar2=None,
                                                           op0=ALU.mult), reads=[S["stb"][g], cb], writes=[S["dgb"][g]])
                for k4 in range(8):
                    bk = bank()
                    for j in range(4):
                        k = k4 * 4 + j
                        P.op("pe", lambda e, bk=bk, j=j, k=k, g=g: e.matmul(
                            bk.t[:, j * 128:(j + 1) * 128], lhsT=xt[:, g, k * 128:(k + 1) * 128], rhs=dg[:, g, :],
                            start=True, stop=True), reads=[S["xtb"][g], S["dgb"][g]], writes=[bk.buf])
                    for j in range(4):
                        k = k4 * 4 + j
                        P.op("act", lambda e, bk=bk, j=j, k=k, g=g: e.activation(
                            out=xn[:, k, g * 128:(g + 1) * 128], in_=bk.t[:, j * 128:(j + 1) * 128], func=AF.Identity,
                            scale=A[:, k:k + 1], bias=B[:, k:k + 1]), reads=[bk.buf, mvb], writes=[S["xnb"][k]])

        def evac_add(S, bk, dch, Gv, T):
            ng = T // 128
            i = S["rr"][0] % 3
            S["rr"][0] += 1
            o32 = S["o32"][i]
            xt = S["xt"]
            P.op("act", lambda e: e.activation(out=o32[:, :T], in_=bk.t[:, :T], func=AF.Identity, scale=Gv[:, dch:dch + 1]),
                 reads=[bk.buf, mvb], writes=[S["o32b"][i]])
            tb = bank()
            for g in range(ng):
                P.op("pe", lambda e, g=g: e.matmul(tb.t[:, g * 128:(g + 1) * 128], lhsT=o32[:, g * 128:(g + 1) * 128],
                                                   rhs=ident[:], start=True, stop=True),
                     reads=[S["o32b"][i], cb], writes=[tb.buf])
            P.op("dve", lambda e: e.tensor_tensor(out=V(xt, dch * 128, [[D, ng], [1, 128]]), in0=V(xt, dch * 128, [[D, ng], [1, 128]]),
                                                  in1=V(tb.t, 0, [[128, ng], [1, 128]]), op=ALU.add),
                 reads=[tb.buf] + S["xtb"][:ng], writes=S["xtb"][:ng])

        def hslot(S, fl):
            return S["work"][:, fl * 512:(fl + 1) * 512], S["wkb"][(fl * 512) // 4096]

        def mm_group(S, W, r0, nkt, c0, ncols, rhs_fn, T, bks):
            nf = ncols // 128
            k0 = 0
            while k0 < nkt:
                nk = min(8, nkt - k0)
                wt, wbuf = stage(W, r0 + k0 * 128, nk, c0, ncols)
                for kk in range(nk):
                    k = k0 + kk
                    rap, rbuf = rhs_fn(k)
                    for cc in range(nf):
                        P.op("pe", lambda e, wt=wt, kk=kk, cc=cc, rap=rap, k=k: e.matmul(
                            bks[cc].t[:, :T], lhsT=wt[:, kk, cc * 128:(cc + 1) * 128], rhs=rap,
                            start=(k == 0), stop=(k == nkt - 1)), reads=[wbuf, rbuf], writes=[bks[cc].buf])
                k0 += nk

        def ffn(S, Wi, Wo, Gv, T):
            xn = S["xn"]
            for (f0, f1) in ((0, 44), (44, 86)):
                for fp in range(f0, f1, 2):
                    nf = min(2, f1 - fp)
                    acc = {}
                    for which, cbase in (("g", 0), ("u", FF)):
                        bks = [bank() for _ in range(nf)]
                        mm_group(S, Wi, 0, 32, cbase + fp * 128, nf * 128, lambda k: (xn[:, k, :T], S["xnb"][k]), T, bks)
                        acc[which] = bks
                    for cc in range(nf):
                        i = S["rr"][1] % 2
                        S["rr"][1] += 1
                        sg = S["sg"][i]
                        hap, hb = hslot(S, fp + cc - f0)
                        P.op("act", lambda e, sg=sg, b=acc["g"][cc]: e.activation(out=sg[:, :T], in_=b.t[:, :T], func=AF.Silu),
                             reads=[acc["g"][cc].buf], writes=[S["sgb"][i]])
                        P.op("dve", lambda e, sg=sg, b=acc["u"][cc], hap=hap: e.tensor_tensor(
                            out=hap[:, :T], in0=sg[:, :T], in1=b.t[:, :T], op=ALU.mult),
                            reads=[S["sgb"][i], acc["u"][cc].buf], writes=[hb])
                nfl = f1 - f0
                for dp in range(16):
                    bks = [bank(), bank()]

                    def rh(k):
                        a, b = hslot(S, k)
                        return a[:, :T], b
                    mm_group(S, Wo, f0 * 128, nfl, dp * 256, 256, rh, T, bks)
                    for cc in range(2):
                        evac_add(S, bks[cc], dp * 2 + cc, Gv, T)

        usb_, upb_, x1b_, zb_, pmb_ = Buf(), Buf(), Buf(), Buf(), Buf()
        with ExitStack() as p1:
            S = tile_phase(p1, "p1")
            tiles = [("c", 0, NCTX)] + [("x", o, 512) for o in range(0, NSEQ, 512)]
            for (kind, o, T) in tiles:
                ng = T // 128
                j = 1 if kind == "c" else 0
                src = ctx_d if kind == "c" else x_d[o:o + T, :]
                load_x(S, src, ng)
                norm_to_xn(S, ng, mv(j, 0, 0), mv(j, 1, 0))
                ffn(S, f1i_d, f1o_d, mv(j, 2, 0), T)
                if kind == "x":
                    store_x(S, x1_d[o:o + T, :], ng, [x1b_])
                norm_to_xn(S, ng, mv(j, 0, 1), mv(j, 1, 1))
                ncp = 4 if kind == "c" else 12
                col0 = 0 if kind == "c" else NCTX + o
                xn = S["xn"]
                for cp in range(ncp):
                    bks = [bank(), bank()]
                    mm_group(S, win_d, 0, 32, cp * 256, 256, lambda k: (xn[:, k, :T], S["xnb"][k]), T, bks)
                    for cc in range(2):
                        ch = cp * 2 + cc
                        i = S["rr"][0] % 3
                        S["rr"][0] += 1
                        o32 = S["o32"][i]
                        P.op("dve", lambda e, o32=o32, b=bks[cc]: e.tensor_copy(out=o32[:, :T], in_=b.t[:, :T]),
                             reads=[bks[cc].buf], writes=[S["o32b"][i]])
                        if ch < 8:
                            dst = us_d[ch * 128:(ch + 1) * 128, col0:col0 + T]
                            wbf = usb_
                        else:
                            dst = up_d[(ch - 8) * 128:(ch - 7) * 128, o:o + T]
                            wbf = upb_
                        P.dma("sp", S["o32s"][i], lambda e, o32=o32, dst=dst: [e.dma_start(out=dst, in_=o32[:, :T])],
                              reads=[S["o32b"][i]], writes=[wbf])
            P.barrier()

        with ExitStack() as p2:
            J = SSM_J
            LB = LT + NCTX
            LTJ = LT // J
            NCHJ = (LTJ + 511) // 512
            NB = 512 // J
            CPt = sbuf(p2, "CPt", [128, 2048], F32)
            bmask = sbuf(p2, "bmask", [128, 8], F32)
            NS_ = 16 + 2 * (J + 1)
            prm = sbuf(p2, "prm", [128, NS_ * 64], F32)
            CPb = [sbuf(p2, "CPb%d" % n, [128, 2048], BF16) for n in range(J)]
            BPb = [sbuf(p2, "BPb%d" % n, [128, 2048], BF16) for n in range(J)]
            Xm1c = sbuf(p2, "Xm1c", [128, 2048], BF16)
            p2s = ExitStack()
            SPt = sbuf(p2s, "SPt", [128, 192], F32)
            BPt = sbuf(p2s, "BPt", [128, 5120], F32)
            BSt = sbuf(p2s, "BSt", [128, 2048], F32)
            tmpA = sbuf(p2s, "tmpA", [128, 12 * 1024], F32)
            tmpI = sbuf(p2s, "tmpI", [128, 1024], I32)
            bbar = sbuf(p2s, "bbar", [128, 2048], F32)
            bbS = sbuf(p2s, "bbS", [128, 2048], F32)
            tS = [sbuf(p2s, "tS%d" % i, [128, 2048], F32) for i in range(2)]
            pb = Buf("p2params")
            psem = P.dma_sem()
            for t_, s_ in ((SPt, sp_d), (BPt, bp_d), (CPt, cp_d), (bmask, bm_d), (BSt, bs_d)):
                P.dma("sp", psem, lambda e, t_=t_, s_=s_: [e.dma_start(out=t_[:], in_=s_)], writes=[pb])

            def dv(fn, r=(pb,), w=(pb,)):
                P.op("dve", fn, reads=list(r), writes=list(w))

            def ac(fn, r=(pb,), w=(pb,)):
                P.op("act", fn, reads=list(r), writes=list(w))

            def fold(ang, kf, bufs):
                dv(lambda e: e.tensor_scalar(out=kf, in0=ang, scalar1=PI, scalar2=-2 * PI, op0=ALU.is_gt, op1=ALU.mult), bufs, bufs)
                dv(lambda e: e.tensor_tensor(out=ang, in0=ang, in1=kf, op=ALU.add), bufs, bufs)
                dv(lambda e: e.tensor_scalar(out=kf, in0=ang, scalar1=-PI, scalar2=2 * PI, op0=ALU.is_lt, op1=ALU.mult), bufs, bufs)
                dv(lambda e: e.tensor_tensor(out=ang, in0=ang, in1=kf, op=ALU.add), bufs, bufs)

            def sincos(dst_s, dst_c, ang, kf, ki, bufs):
                dv(lambda e: e.tensor_scalar(out=ki, in0=ang, scalar1=1.0 / (2 * PI), scalar2=None, op0=ALU.mult), bufs, bufs)
                dv(lambda e: e.tensor_copy(out=kf, in_=ki), bufs, bufs)
                dv(lambda e: e.scalar_tensor_tensor(out=ang, in0=kf, scalar=-2 * PI, in1=ang, op0=ALU.mult, op1=ALU.add), bufs, bufs)
                fold(ang, kf, bufs)
                ac(lambda e: e.activation(out=dst_s, in_=ang, func=AF.Sin), bufs, bufs)
                dv(lambda e: e.tensor_scalar(out=ang, in0=ang, scalar1=PI / 2, scalar2=None, op0=ALU.add), bufs, bufs)
                fold(ang, kf, bufs)
                ac(lambda e: e.activation(out=dst_c, in_=ang, func=AF.Sin), bufs, bufs)

            def cmul(o_re, o_im, a_re, a_im, b_re, b_im, t_a, t_b, neg_im=False):
                dv(lambda e: e.tensor_tensor(out=t_a, in0=a_re, in1=b_re, op=ALU.mult))
                dv(lambda e: e.tensor_tensor(out=t_b, in0=a_im, in1=b_im, op=ALU.mult))
                dv(lambda e: e.tensor_tensor(out=o_re, in0=t_a, in1=t_b, op=ALU.subtract))
                dv(lambda e: e.tensor_tensor(out=t_a, in0=a_re, in1=b_im, op=ALU.mult))
                dv(lambda e: e.tensor_tensor(out=t_b, in0=a_im, in1=b_re, op=ALU.mult))
                if neg_im:
                    dv(lambda e: e.scalar_tensor_tensor(out=o_im, in0=t_a, scalar=-1.0, in1=t_b, op0=ALU.mult, op1=ALU.subtract))
                else:
                    dv(lambda e: e.tensor_tensor(out=o_im, in0=t_a, in1=t_b, op=ALU.add))

            def spv(c):
                return V(SPt, c * 32, [[96, 2], [1, 32]])

            def pv(i):
                return V(prm, i * 64, [[32, 2], [1, 32]])
            (LNR, TH, L1R, L1I, CRE, CIM, LNRJ, THJ, RJ, ZR, ZI, IR, II, TA, TB, TC) = range(16)

            def LnR(n):
                return 16 + 2 * n

            def LnI(n):
                return 17 + 2 * n
            ki64 = V(tmpI, 0, [[32, 2], [1, 32]])
            dv(lambda e: e.tensor_scalar(out=pv(TA), in0=spv(0), scalar1=-1e-4, scalar2=None, op0=ALU.min))
            ac(lambda e: e.activation(out=pv(TB), in_=spv(2), func=AF.Exp))
            dv(lambda e: e.tensor_tensor(out=pv(LNR), in0=pv(TA), in1=pv(TB), op=ALU.mult))
            dv(lambda e: e.tensor_tensor(out=pv(TH), in0=spv(1), in1=pv(TB), op=ALU.mult))
            ac(lambda e: e.activation(out=pv(TB), in_=pv(LNR), func=AF.Exp))
            dv(lambda e: e.tensor_copy(out=pv(TC), in_=pv(TH)))
            sincos(pv(L1I), pv(L1R), pv(TC), pv(ZR), ki64, [pb])
            dv(lambda e: e.tensor_tensor(out=pv(L1R), in0=pv(L1R), in1=pv(TB), op=ALU.mult))
            dv(lambda e: e.tensor_tensor(out=pv(L1I), in0=pv(L1I), in1=pv(TB), op=ALU.mult))
            dv(lambda e: e.tensor_tensor(out=pv(TC), in0=pv(TB), in1=pv(TB), op=ALU.mult))
            dv(lambda e: e.reciprocal(out=pv(TC), in_=pv(TC)))
            dv(lambda e: e.tensor_tensor(out=pv(IR), in0=pv(L1R), in1=pv(TC), op=ALU.mult))
            dv(lambda e: e.scalar_tensor_tensor(out=pv(II), in0=pv(L1I), scalar=-1.0, in1=pv(TC), op0=ALU.mult, op1=ALU.mult))
            dv(lambda e: e.tensor_tensor(out=pv(TB), in0=pv(TA), in1=pv(TA), op=ALU.mult))
            dv(lambda e: e.tensor_tensor(out=pv(TC), in0=spv(1), in1=spv(1), op=ALU.mult))
            dv(lambda e: e.tensor_tensor(out=pv(TB), in0=pv(TB), in1=pv(TC), op=ALU.add))
            dv(lambda e: e.reciprocal(out=pv(TB), in_=pv(TB)))
            dv(lambda e: e.tensor_scalar(out=pv(ZR), in0=pv(L1R), scalar1=-1.0, scalar2=None, op0=ALU.add))
            dv(lambda e: e.tensor_tensor(out=pv(CRE), in0=pv(ZR), in1=pv(TA), op=ALU.mult))
            dv(lambda e: e.tensor_tensor(out=pv(TC), in0=pv(L1I), in1=spv(1), op=ALU.mult))
            dv(lambda e: e.tensor_tensor(out=pv(CRE), in0=pv(CRE), in1=pv(TC), op=ALU.add))
            dv(lambda e: e.tensor_tensor(out=pv(CRE), in0=pv(CRE), in1=pv(TB), op=ALU.mult))
            dv(lambda e: e.tensor_tensor(out=pv(CIM), in0=pv(L1I), in1=pv(TA), op=ALU.mult))
            dv(lambda e: e.tensor_tensor(out=pv(TC), in0=pv(ZR), in1=spv(1), op=ALU.mult))
            dv(lambda e: e.tensor_tensor(out=pv(CIM), in0=pv(CIM), in1=pv(TC), op=ALU.subtract))
            dv(lambda e: e.tensor_tensor(out=pv(CIM), in0=pv(CIM), in1=pv(TB), op=ALU.mult))
            dv(lambda e: e.memset(pv(LnR(0)), 1.0))
            dv(lambda e: e.memset(pv(LnI(0)), 0.0))
            for n in range(1, J + 1):
                cmul(pv(LnR(n)), pv(LnI(n)), pv(LnR(n - 1)), pv(LnI(n - 1)), pv(L1R), pv(L1I), pv(TA), pv(TB))
            dv(lambda e: e.tensor_scalar(out=pv(LNRJ), in0=pv(LNR), scalar1=float(J), scalar2=None, op0=ALU.mult))
            dv(lambda e: e.tensor_scalar(out=pv(THJ), in0=pv(TH), scalar1=float(J), scalar2=None, op0=ALU.mult))
            ac(lambda e: e.activation(out=pv(RJ), in_=pv(LNRJ), func=AF.Exp))
            dv(lambda e: e.tensor_scalar(out=pv(TC), in0=pv(THJ), scalar1=512.0, scalar2=None, op0=ALU.mult))
            sincos(pv(ZI), pv(ZR), pv(TC), pv(TA), ki64, [pb])

            def sarr(t, ri):
                return V(t, ri * 512, [[1024, 2], [16, 32], [1, 16]])

            def pbc(i):
                return V(prm, i * 64, [[32, 2], [1, 32], [0, 16]])
            cmul(sarr(bbS, 0), sarr(bbS, 1), pbc(CRE), pbc(CIM), sarr(BSt, 0), sarr(BSt, 1), sarr(tS[0], 0), sarr(tS[0], 1))
            cmul(sarr(Xm1c, 0), sarr(Xm1c, 1), pbc(IR), pbc(II), sarr(bbS, 0), sarr(bbS, 1), sarr(tS[0], 0), sarr(tS[0], 1))
            for n in range(1, J + 1):
                cmul(sarr(CPb[n - 1], 0), sarr(CPb[n - 1], 1), pbc(LnR(n)), pbc(LnI(n)), sarr(CPt, 0), sarr(CPt, 1),
                     sarr(tS[0], 0), sarr(tS[0], 1), neg_im=True)

            def bpv(c):
                return V(BPt, c * 512, [[2560, 2], [1, 512]])

            def ta(i):
                return V(tmpA, i * 1024, [[512, 2], [1, 512]])
            kiB = V(tmpI, 0, [[512, 2], [1, 512]])
            lre, stp, lnr, th, er, sn, cs, kf, lbr, lbi, pwr, pwi = [ta(i) for i in range(12)]
            dv(lambda e: e.tensor_scalar(out=lre, in0=bpv(0), scalar1=-1e-4, scalar2=None, op0=ALU.min))
            ac(lambda e: e.activation(out=stp, in_=bpv(2), func=AF.Exp))
            dv(lambda e: e.tensor_tensor(out=lnr, in0=lre, in1=stp, op=ALU.mult))
            dv(lambda e: e.tensor_tensor(out=th, in0=bpv(1), in1=stp, op=ALU.mult))
            ac(lambda e: e.activation(out=er, in_=lnr, func=AF.Exp))
            sincos(sn, cs, th, kf, kiB, [pb])
            dv(lambda e: e.tensor_tensor(out=lbr, in0=er, in1=cs, op=ALU.mult))
            dv(lambda e: e.tensor_tensor(out=lbi, in0=er, in1=sn, op=ALU.mult))
            nre, nim = stp, lnr
            dv(lambda e: e.tensor_scalar(out=nre, in0=lbr, scalar1=-1.0, scalar2=None, op0=ALU.add))
            dv(lambda e: e.tensor_copy(out=nim, in_=lbi))
            den, tq = er, cs
            dv(lambda e: e.tensor_tensor(out=den, in0=lre, in1=lre, op=ALU.mult))
            dv(lambda e: e.tensor_tensor(out=tq, in0=bpv(1), in1=bpv(1), op=ALU.mult))
            dv(lambda e: e.tensor_tensor(out=den, in0=den, in1=tq, op=ALU.add))
            dv(lambda e: e.reciprocal(out=den, in_=den))
            cre, cim = sn, th
            dv(lambda e: e.tensor_tensor(out=cre, in0=nre, in1=lre, op=ALU.mult))
            dv(lambda e: e.tensor_tensor(out=tq, in0=nim, in1=bpv(1), op=ALU.mult))
            dv(lambda e: e.tensor_tensor(out=cre, in0=cre, in1=tq, op=ALU.add))
            dv(lambda e: e.tensor_tensor(out=cre, in0=cre, in1=den, op=ALU.mult))
            dv(lambda e: e.tensor_tensor(out=cim, in0=nim, in1=lre, op=ALU.mult))
            dv(lambda e: e.tensor_tensor(out=tq, in0=nre, in1=bpv(1), op=ALU.mult))
            dv(lambda e: e.tensor_tensor(out=cim, in0=cim, in1=tq, op=ALU.subtract))
            dv(lambda e: e.tensor_tensor(out=cim, in0=cim, in1=den, op=ALU.mult))
            bb_re = V(bbar, 0, [[1024, 2], [1, 512]])
            bb_im = V(bbar, 512, [[1024, 2], [1, 512]])
            cmul(bb_re, bb_im, cre, cim, bpv(3), bpv(4), tq, kf)

            def barr(t, ri):
                return V(t, ri * 512, [[1024, 2], [1, 512]])
            dv(lambda e: e.tensor_copy(out=barr(BPb[0], 0), in_=bb_re))
            dv(lambda e: e.tensor_copy(out=barr(BPb[0], 1), in_=bb_im))
            dv(lambda e: e.tensor_copy(out=pwr, in_=lbr))
            dv(lambda e: e.tensor_copy(out=pwi, in_=lbi))
            for n in range(1, J):
                cmul(barr(BPb[n], 0), barr(BPb[n], 1), pwr, pwi, bb_re, bb_im, tq, kf)
                if n < J - 1:
                    cmul(den, stp, pwr, pwi, lbr, lbi, tq, kf)
                    dv(lambda e: e.tensor_copy(out=pwr, in_=den))
                    dv(lambda e: e.tensor_copy(out=pwi, in_=stp))
            P.barrier()
            p2s.close()

            usf = sbuf(p2, "usf", [128, LB], F32)
            usbf = sbuf(p2, "usbf", [128, LB], BF16)
            BtJ = sbuf(p2, "BtJ", [128, J * 4 * 512], BF16)
            CtJ = sbuf(p2, "CtJ", [128, J * 4 * 512], BF16)
            Xt = sbuf(p2, "Xt", [128, 4 * 512], BF16)
            NK = 2 * J - 1
            Kt = sbuf(p2, "Kt", [128, NK * 128], BF16)
            qre = sbuf(p2, "qre", [128, LTJ], F32)
            qim = sbuf(p2, "qim", [128, LTJ], F32)
            Hh = sbuf(p2, "Hh", [128, 16 * LTJ], BF16)
            t1 = [sbuf(p2, "t1_%d" % i, [128, 512], F32) for i in range(6)]
            tabS = sbuf(p2, "tabS", [128, 512], F32)
            tabC = sbuf(p2, "tabC", [128, 512], F32)
            tabR = sbuf(p2, "tabR", [128, 512], F32)
            tang = sbuf(p2, "tang", [128, 512], F32)
            tkf = sbuf(p2, "tkf", [128, 512], F32)
            tki = sbuf(p2, "tki", [128, 512], I32)
            qin = sbuf(p2, "qin", [128, 32], F32)
            vv = sbuf(p2, "vv", [128, 8], F32)
            yo = [sbuf(p2, "yo%d" % i, [128, 512], F32) for i in range(2)]
            gt = [sbuf(p2, "gt%d" % i, [128, 512], F32) for i in range(2)]
            zt = [sbuf(p2, "zt%d" % i, [128, 512], BF16) for i in range(2)]
            usB, ubB = Buf(), Buf()
            ussem = P.dma_sem()
            zsem = [P.dma_sem(), P.dma_sem()]
            btb, ctb, xtb2, ktb = Buf(), Buf(), Buf(), Buf()
            qb = [Buf(), Buf()]
            hB = [Buf() for _ in range(16)]
            tbB = Buf()
            t1b = [Buf() for _ in range(6)]
            qnb = Buf()
            yob = [Buf(), Buf()]
            gtb = [Buf(), Buf()]
            ztb = [Buf(), Buf()]
            t1rr = [0]
            yrr = [0]

            def nt1():
                i = t1rr[0] % 6
                t1rr[0] += 1
                return t1[i], t1b[i]

            def hv(s, d, ri):
                o = ((s * 2 + d) * 2 + ri) * LTJ
                return o, hB[(s * 2 + d) * 2 + ri]

            def expand(dst, dst_off, src, src_off, wbuf):
                P.op("pool", lambda e: e.tensor_tensor(out=V(dst, dst_off, [[128, 4], [16, 8], [1, 16]]),
                                                       in0=V(src, src_off, [[16, 4], [0, 8], [1, 16]]),
                                                       in1=V(bmask, 0, [[0, 4], [1, 8], [0, 16]]), op=ALU.mult),
                     reads=[pb], writes=[wbuf])

            for t in range(8):
                P.dma("sp", ussem, lambda e, t=t: [e.dma_start(out=usf[:, 0:LT], in_=us_d[t * 128:(t + 1) * 128, :]),
                                                    e.dma_start(out=usf[:, LT:LB], in_=us_d[t * 128:(t + 1) * 128, 0:NCTX])],
                      reads=[usb_], writes=[usB], ndma=2)
                P.op("pool", lambda e: e.tensor_copy(out=usbf[:], in_=usf[:]), reads=[usB], writes=[ubB])
                for d in range(2):
                    for ri in range(2):
                        so = (d * 2 + ri) * 512 + t * 64
                        for j in range(J):
                            expand(BtJ, ((j * 2 + d) * 2 + ri) * 512, BPb[J - 1 - j], so, btb)
                            expand(CtJ, ((j * 2 + d) * 2 + ri) * 512, CPb[j], so, ctb)
                        expand(Xt, (d * 2 + ri) * 512, Xm1c, so, xtb2)
                kbk = None
                for ki_ in range(NK):
                    if ki_ % 4 == 0:
                        kbk = bank()
                    if ki_ == 0:
                        combos = [(0, 0), (1, 0)]
                    elif ki_ < J:
                        combos = [(0, ki_)]
                    else:
                        combos = [(1, ki_ - J + 1)]
                    mm = [(d, tau, ri, s) for (d, tau) in combos for ri in range(2) for s in range(4)]
                    for n_, (d, tau, ri, s) in enumerate(mm):
                        xo = (d * 2 + ri) * 512 + s * 128
                        co = ((tau * 2 + d) * 2 + ri) * 512 + s * 128
                        P.op("pe", lambda e, kbk=kbk, ki_=ki_, xo=xo, co=co, n_=n_, last=(n_ == len(mm) - 1): e.matmul(
                            kbk.t[:, (ki_ % 4) * 128:(ki_ % 4 + 1) * 128], lhsT=Xt[:, xo:xo + 128], rhs=CtJ[:, co:co + 128],
                            start=(n_ == 0), stop=last), reads=[xtb2, ctb], writes=[kbk.buf])
                    P.op("act", lambda e, kbk=kbk, ki_=ki_: e.copy(out=Kt[:, ki_ * 128:(ki_ + 1) * 128],
                                                                  in_=kbk.t[:, (ki_ % 4) * 128:(ki_ % 4 + 1) * 128]),
                         reads=[kbk.buf], writes=[ktb])
                for s in range(4):
                    u = t * 4 + s
                    for d in range(2):
                        pcol = d * 32 + u

                        def pc(i, pcol=pcol):
                            return prm[:, i * 64 + pcol:i * 64 + pcol + 1]
                        bufs = [tbB]
                        P.op("dve", lambda e, pc=pc: e.tensor_scalar(out=tang[:], in0=iota[:, 0:512], scalar1=pc(THJ), scalar2=None,
                                                                     op0=ALU.mult), reads=[pb, cb] + bufs, writes=bufs)
                        sincos(tabS[:], tabC[:], tang[:], tkf[:], tki[:], bufs)
                        P.op("act", lambda e, pc=pc: e.activation(out=tabR[:], in_=iota[:, 1:513], func=AF.Exp, scale=pc(LNRJ)),
                             reads=[pb, cb] + bufs, writes=bufs)
                        for c in range(NCHJ):
                            m0 = c * 512
                            L = min(512, LTJ - m0)
                            pr, pi_ = bank(), bank()
                            for j in range(J):
                                if d == 0:
                                    st_ = m0 * J + j
                                    rhs = usbf[:, st_:st_ + (L - 1) * J + 1:J]
                                else:
                                    st_ = LB - 1 - (m0 * J + j)
                                    sp_ = st_ - (L - 1) * J - 1
                                    rhs = usbf[:, st_:sp_:-J] if sp_ >= 0 else usbf[:, st_::-J]
                                for ri, pk in ((0, pr), (1, pi_)):
                                    bo = ((j * 2 + d) * 2 + ri) * 512 + s * 128
                                    P.op("pe", lambda e, pk=pk, rhs=rhs, L=L, bo=bo, j=j: e.matmul(
                                        pk.t[:, :L], lhsT=BtJ[:, bo:bo + 128], rhs=rhs, start=(j == 0), stop=(j == J - 1)),
                                        reads=[btb, ubB], writes=[pk.buf])
                            a1, a1b = nt1()
                            a2, a2b = nt1()
                            P.op("dve", lambda e, a1=a1, pr=pr, L=L: e.tensor_tensor(out=a1[:, :L], in0=pr.t[:, :L], in1=tabC[:, :L], op=ALU.mult),
                                 reads=[pr.buf, tbB], writes=[a1b])
                            P.op("dve", lambda e, a2=a2, pi_=pi_, L=L: e.tensor_tensor(out=a2[:, :L], in0=pi_.t[:, :L], in1=tabS[:, :L], op=ALU.mult),
                                 reads=[pi_.buf, tbB], writes=[a2b])
                            P.op("pool", lambda e, a1=a1, a2=a2, L=L, m0=m0: e.tensor_tensor(out=qre[:, m0:m0 + L], in0=a1[:, :L], in1=a2[:, :L], op=ALU.add),
                                 reads=[a1b, a2b], writes=[qb[0]])
                            a3, a3b = nt1()
                            a4, a4b = nt1()
                            P.op("dve", lambda e, a3=a3, pi_=pi_, L=L: e.tensor_tensor(out=a3[:, :L], in0=pi_.t[:, :L], in1=tabC[:, :L], op=ALU.mult),
                                 reads=[pi_.buf, tbB], writes=[a3b])
                            P.op("dve", lambda e, a4=a4, pr=pr, L=L: e.tensor_tensor(out=a4[:, :L], in0=pr.t[:, :L], in1=tabS[:, :L], op=ALU.mult),
                                 reads=[pr.buf, tbB], writes=[a4b])
                            P.op("pool", lambda e, a3=a3, a4=a4, L=L, m0=m0: e.tensor_tensor(out=qim[:, m0:m0 + L], in0=a3[:, :L], in1=a4[:, :L], op=ALU.subtract),
                                 reads=[a3b, a4b], writes=[qb[1]])
                            for ri, q in ((0, qre), (1, qim)):
                                P.op("dve", lambda e, q=q, m0=m0, L=L, pc=pc: e.tensor_tensor_scan(
                                    out=q[:, m0:m0 + L], data0=pc(RJ).to_broadcast([128, L]), data1=q[:, m0:m0 + L], initial=0.0,
                                    op0=ALU.mult, op1=ALU.add), reads=[qb[ri], pb], writes=[qb[ri]])

                        def qn(ri, c):
                            return qin[:, ri * 16 + c:ri * 16 + c + 1]
                        for c in range(1, NCHJ):
                            er_ = qre[:, c * 512 - 1:c * 512]
                            ei_ = qim[:, c * 512 - 1:c * 512]
                            rds = [qb[0], qb[1], pb, qnb]
                            dv(lambda e, ei_=ei_, pc=pc: e.tensor_scalar(out=vv[:, 0:1], in0=ei_, scalar1=pc(ZI), scalar2=None, op0=ALU.mult), rds, [qnb])
                            dv(lambda e, er_=er_, pc=pc, c=c, qn=qn: e.scalar_tensor_tensor(out=qn(0, c), in0=er_, scalar=pc(ZR), in1=vv[:, 0:1], op0=ALU.mult, op1=ALU.subtract), rds, [qnb])
                            dv(lambda e, er_=er_, pc=pc: e.tensor_scalar(out=vv[:, 1:2], in0=er_, scalar1=pc(ZI), scalar2=None, op0=ALU.mult), rds, [qnb])
                            dv(lambda e, ei_=ei_, pc=pc, c=c, qn=qn: e.scalar_tensor_tensor(out=qn(1, c), in0=ei_, scalar=pc(ZR), in1=vv[:, 1:2], op0=ALU.mult, op1=ALU.add), rds, [qnb])
                            m0 = c * 512
                            L = min(512, LTJ - m0)
                            for ri, q in ((0, qre), (1, qim)):
                                P.op("dve", lambda e, q=q, m0=m0, L=L, ri=ri, c=c, qn=qn: e.scalar_tensor_tensor(
                                    out=q[:, m0:m0 + L], in0=tabR[:, :L], scalar=qn(ri, c), in1=q[:, m0:m0 + L],
                                    op0=ALU.mult, op1=ALU.add), reads=[qnb, tbB, qb[ri]], writes=[qb[ri]])
                        o_re, hbr = hv(s, d, 0)
                        o_im, hbi = hv(s, d, 1)
                        for c in range(NCHJ):
                            m0 = c * 512
                            L = min(512, LTJ - m0)
                            a1, a1b = nt1()
                            a2, a2b = nt1()
                            P.op("pool", lambda e, a1=a1, L=L, m0=m0: e.tensor_tensor(out=a1[:, :L], in0=qre[:, m0:m0 + L], in1=tabC[:, :L], op=ALU.mult),
                                 reads=[qb[0], tbB], writes=[a1b])
                            P.op("pool", lambda e, a2=a2, L=L, m0=m0: e.tensor_tensor(out=a2[:, :L], in0=qim[:, m0:m0 + L], in1=tabS[:, :L], op=ALU.mult),
                                 reads=[qb[1], tbB], writes=[a2b])
                            P.op("pool", lambda e, a1=a1, a2=a2, L=L, m0=m0, o_re=o_re: e.tensor_tensor(out=Hh[:, o_re + m0:o_re + m0 + L], in0=a1[:, :L], in1=a2[:, :L], op=ALU.subtract),
                                 reads=[a1b, a2b], writes=[hbr])
                            a3, a3b = nt1()
                            a4, a4b = nt1()
                            P.op("dve", lambda e, a3=a3, L=L, m0=m0: e.tensor_tensor(out=a3[:, :L], in0=qim[:, m0:m0 + L], in1=tabC[:, :L], op=ALU.mult),
                                 reads=[qb[1], tbB], writes=[a3b])
                            P.op("dve", lambda e, a4=a4, L=L, m0=m0: e.tensor_tensor(out=a4[:, :L], in0=qre[:, m0:m0 + L], in1=tabS[:, :L], op=ALU.mult),
                                 reads=[qb[0], tbB], writes=[a4b])
                            P.op("dve", lambda e, a3=a3, a4=a4, L=L, m0=m0, o_im=o_im: e.tensor_tensor(out=Hh[:, o_im + m0:o_im + m0 + L], in0=a3[:, :L], in1=a4[:, :L], op=ALU.add),
                                 reads=[a3b, a4b], writes=[hbi])
                for blk in range(NSEQ // 512):
                    yb = bank()
                    P.op("dve", lambda e, yb=yb: e.memset(yb.t[:], 0.0), writes=[yb.buf])
                    mf = (NCTX + blk * 512) // J
                    mbs = NCTX // J + NSEQ // J - 2 - blk * NB
                    for j in range(J):
                        oap = yb.t[:, j:j + (NB - 1) * J + 1:J]
                        for s in range(4):
                            for ri in range(2):
                                of, hbf = hv(s, 0, ri)
                                co = ((j * 2 + 0) * 2 + ri) * 512 + s * 128
                                P.op("pe", lambda e, oap=oap, co=co, of=of, mf=mf: e.matmul(
                                    oap, lhsT=CtJ[:, co:co + 128], rhs=Hh[:, of + mf - 1:of + mf - 1 + NB], start=False, stop=False),
                                    reads=[ctb, hbf], writes=[yb.buf])
                                ob, hbb = hv(s, 1, ri)
                                co = (((J - 1 - j) * 2 + 1) * 2 + ri) * 512 + s * 128
                                P.op("pe", lambda e, oap=oap, co=co, ob=ob, mbs=mbs: e.matmul(
                                    oap, lhsT=CtJ[:, co:co + 128], rhs=Hh[:, ob + mbs:ob + mbs - NB:-1], start=False, stop=False),
                                    reads=[ctb, hbb], writes=[yb.buf])
                        for i in range(J):
                            if i == j:
                                ko = 0
                            elif i < j:
                                ko = j - i
                            else:
                                ko = J - 1 + (i - j)
                            u0 = NCTX + blk * 512 + i
                            P.op("pe", lambda e, oap=oap, ko=ko, u0=u0: e.matmul(
                                oap, lhsT=Kt[:, ko * 128:(ko + 1) * 128], rhs=usbf[:, u0:u0 + (NB - 1) * J + 1:J], start=False, stop=False),
                                reads=[ktb, ubB], writes=[yb.buf])
                    i2 = yrr[0] % 2
                    yrr[0] += 1
                    yo_, gt_, zt_ = yo[i2], gt[i2], zt[i2]
                    u0 = NCTX + blk * 512
                    P.op("dve", lambda e, yo_=yo_, yb=yb, u0=u0, t=t: e.scalar_tensor_tensor(
                        out=yo_[:], in0=usf[:, u0:u0 + 512], scalar=sdT[:, t:t + 1], in1=yb.t[:], op0=ALU.mult, op1=ALU.add),
                        reads=[usB, yb.buf, cb], writes=[yob[i2]])
                    P.op("pool", lambda e, yo_=yo_, gt_=gt_: e.tensor_tensor(out=gt_[:], in0=yo_[:], in1=yo_[:], op=ALU.mult), reads=[yob[i2]], writes=[gtb[i2]])
                    P.op("pool", lambda e, gt_=gt_: e.tensor_scalar(out=gt_[:], in0=gt_[:], scalar1=0.044715, scalar2=1.0, op0=ALU.mult, op1=ALU.add),
                         reads=[gtb[i2]], writes=[gtb[i2]])
                    P.op("pool", lambda e, yo_=yo_, gt_=gt_: e.tensor_tensor(out=gt_[:], in0=gt_[:], in1=yo_[:], op=ALU.mult), reads=[gtb[i2], yob[i2]], writes=[gtb[i2]])
                    P.op("act", lambda e, gt_=gt_: e.activation(out=gt_[:], in_=gt_[:], func=AF.Sigmoid, scale=1.5957691216057308), reads=[gtb[i2]], writes=[gtb[i2]])
                    P.op("dve", lambda e, yo_=yo_, gt_=gt_, zt_=zt_: e.tensor_tensor(out=zt_[:], in0=gt_[:], in1=yo_[:], op=ALU.mult),
                         reads=[gtb[i2], yob[i2]], writes=[ztb[i2]])
                    P.dma("sp", zsem[i2], lambda e, t=t, blk=blk, zt_=zt_: [e.dma_start(out=z_d[t * 128:(t + 1) * 128, blk * 512:(blk + 1) * 512], in_=zt_[:])],
                          reads=[ztb[i2]], writes=[zb_])
            P.barrier()

        with ExitStack() as p2b:
            upf = sbuf(p2b, "upf", [128, NSEQ], F32)
            XP = sbuf(p2b, "XP", [128, ROWS * 80], F32)
            RP = sbuf(p2b, "RP", [128, (ROWS + 16) * 64], F32)
            AA = [sbuf(p2b, "AA%d" % i, [128, (ROWS + 16) * 80], F32) for i in range(2)]
            pmt = sbuf(p2b, "pmt2", [128, NSEQ], BF16)
            invc = sbuf(p2b, "invc", [128, 4 * 64], F32)
            invr = sbuf(p2b, "invr", [128, 4 * 64], F32)
            tq1 = sbuf(p2b, "tq1", [128, 64], F32)
            upB, xpB, rpB, aB, pmB, ivB = Buf(), Buf(), Buf(), [Buf(), Buf()], Buf(), Buf()
            upsem, pmsem = P.dma_sem(), P.dma_sem()
            P.op("pool", lambda e: e.memset(XP[:], 0.0), writes=[xpB])
            P.op("pool", lambda e: e.memset(RP[:], 0.0), writes=[rpB])
            for wi, w in enumerate((2, 4, 8, 16)):
                hw = w // 2
                for (tab, n) in ((invc, 64), (invr, ROWS)):
                    dst = tab[:, wi * 64:wi * 64 + n]
                    P.op("dve", lambda e, dst=dst, n=n, hw=hw: e.tensor_scalar(out=dst, in0=iota[:, 0:n], scalar1=float(hw), scalar2=float(n),
                                                                               op0=ALU.add, op1=ALU.min), reads=[cb, ivB], writes=[ivB])
                    P.op("dve", lambda e, n=n, hw=hw: e.tensor_scalar(out=tq1[:, 0:n], in0=iota[:, 0:n], scalar1=float(-hw), scalar2=0.0,
                                                                      op0=ALU.add, op1=ALU.max), reads=[cb, ivB], writes=[ivB])
                    P.op("dve", lambda e, dst=dst, n=n: e.tensor_tensor(out=dst, in0=dst, in1=tq1[:, 0:n], op=ALU.subtract), reads=[ivB], writes=[ivB])
                    P.op("dve", lambda e, dst=dst: e.reciprocal(out=dst, in_=dst), reads=[ivB], writes=[ivB])
            for ct in range(16):
                wi = ct // 4
                w = (2, 4, 8, 16)[wi]
                kk_ = wi + 1
                hw = w // 2
                P.dma("sp", upsem, lambda e, ct=ct: [e.dma_start(out=upf[:], in_=up_d[ct * 128:(ct + 1) * 128, :])], reads=[upb_], writes=[upB])
                P.op("pool", lambda e: e.tensor_copy(out=V(XP, 8, [[80, ROWS], [1, 64]]), in_=V(upf, 0, [[64, ROWS], [1, 64]])),
                     reads=[upB], writes=[xpB])
                src, sB, sw = XP, xpB, 80
                for j in range(1, kk_ + 1):
                    sh = 1 << (j - 1)
                    if j < kk_:
                        wdt = 80 - (1 << j) + 1
                        dt_, dB = AA[j % 2], aB[j % 2]
                        P.op("dve", lambda e, src=src, dt_=dt_, sh=sh, wdt=wdt: e.tensor_tensor(
                            out=V(dt_, 0, [[80, ROWS], [1, wdt]]), in0=V(src, 0, [[80, ROWS], [1, wdt]]),
                            in1=V(src, sh, [[80, ROWS], [1, wdt]]), op=ALU.add), reads=[sB], writes=[dB])
                        src, sB = dt_, dB
                    else:
                        o = 8 - hw
                        P.op("dve", lambda e, src=src, sh=sh, o=o: e.tensor_tensor(
                            out=V(RP, 8 * 64, [[64, ROWS], [1, 64]]), in0=V(src, o, [[80, ROWS], [1, 64]]),
                            in1=V(src, o + sh, [[80, ROWS], [1, 64]]), op=ALU.add), reads=[sB], writes=[rpB])
                src, sB = RP, rpB
                for j in range(1, kk_ + 1):
                    sh = 1 << (j - 1)
                    if j < kk_:
                        nr = ROWS + 16 - (1 << j) + 1
                        dt_, dB = AA[j % 2], aB[j % 2]
                        P.op("pool", lambda e, src=src, dt_=dt_, sh=sh, nr=nr: e.tensor_tensor(
                            out=dt_[:, 0:nr * 64], in0=src[:, 0:nr * 64], in1=src[:, sh * 64:(sh + nr) * 64], op=ALU.add),
                            reads=[sB], writes=[dB])
                        src, sB = dt_, dB
                    else:
                        o = (8 - hw) * 64
                        dt_, dB = AA[j % 2], aB[j % 2]
                        P.op("pool", lambda e, src=src, dt_=dt_, sh=sh, o=o: e.tensor_tensor(
                            out=dt_[:, 0:NSEQ], in0=src[:, o:o + NSEQ], in1=src[:, o + sh * 64:o + sh * 64 + NSEQ], op=ALU.add),
                            reads=[sB], writes=[dB])
                        src, sB = dt_, dB
                P.op("dve", lambda e, src=src, wi=wi: e.tensor_tensor(out=V(src, 0, [[64, ROWS], [1, 64]]), in0=V(src, 0, [[64, ROWS], [1, 64]]),
                                                                     in1=V(invc, wi * 64, [[0, ROWS], [1, 64]]), op=ALU.mult),
                     reads=[sB, ivB], writes=[sB])
                P.op("dve", lambda e, src=src, wi=wi: e.tensor_tensor(out=V(src, 0, [[64, ROWS], [1, 64]]), in0=V(src, 0, [[64, ROWS], [1, 64]]),
                                                                     in1=V(invr, wi * 64, [[1, ROWS], [0, 64]]), op=ALU.mult),
                     reads=[sB, ivB], writes=[sB])
                P.op("dve", lambda e, src=src: e.tensor_tensor(out=pmt[:], in0=src[:, 0:NSEQ], in1=upf[:], op=ALU.subtract),
                     reads=[sB, upB], writes=[pmB])
                P.dma("sp", pmsem, lambda e, ct=ct: [e.dma_start(out=pm_d[ct * 128:(ct + 1) * 128, :], in_=pmt[:])], reads=[pmB], writes=[pmb_])
            P.barrier()

        with ExitStack() as p3:
            S = tile_phase(p3, "p3")
            work = S["work"]
            wkb = S["wkb"]
            zsem3, pmsem3, fgsem = P.dma_sem(), P.dma_sem(), P.dma_sem()
            zsem3b, pmsem3b = P.dma_sem(), P.dma_sem()
            x2sem = [P.dma_sem(), P.dma_sem()]
            xn = S["xn"]
            for o in range(0, NLOC, 512):
                T = 512
                ng = 4
                m0, m1 = hsel[:, 0:1], hsel[:, 1:2]
                xt = S["xt"]
                if NLOC == NSEQ:
                    load_x(S, x1_d[o:o + T, :], ng)
                else:
                    for g in range(ng):
                        tv = work[:, (g % 2) * 8192:(g % 2 + 1) * 8192].bitcast(F32)
                        tvb = [wkb[(g % 2) * 2], wkb[(g % 2) * 2 + 1]]
                        r0 = o + g * 128
                        P.dma("sp", S["xts"][g], lambda e, g=g, r0=r0: [e.dma_start(out=xt[:, g, :], in_=x1_d[r0:r0 + 128, :])],
                              reads=[x1b_], writes=[S["xtb"][g]])
                        P.dma("sp", x2sem[g % 2], lambda e, tv=tv, r0=r0: [e.dma_start(out=tv, in_=x1_d[NLOC + r0:NLOC + r0 + 128, :])],
                              reads=[x1b_], writes=tvb)
                        P.op("dve", lambda e, tv=tv: e.tensor_scalar(out=tv, in0=tv, scalar1=m1, scalar2=None, op0=ALU.mult),
                             reads=tvb + [cb], writes=tvb)
                        P.op("dve", lambda e, tv=tv, g=g: e.scalar_tensor_tensor(out=xt[:, g, :], in0=xt[:, g, :], scalar=m0, in1=tv,
                                                                                op0=ALU.mult, op1=ALU.add),
                             reads=tvb + [cb, S["xtb"][g]], writes=[S["xtb"][g]])
                norm_to_xn(S, ng, mv(0, 0, 1), mv(0, 1, 1))
                P.dma("sp", zsem3, lambda e, o=o: [e.dma_start(out=V(work, 0, [[512, 8], [1, 512]]),
                                                               in_=z_d[:, o:o + T].rearrange("(k p) n -> p k n", p=128))],
                      reads=[zb_], writes=[wkb[0]])
                P.dma("sp", pmsem3, lambda e, o=o: [e.dma_start(out=V(work, 4096, [[512, 16], [1, 512]]),
                                                                in_=pm_d[:, o:o + T].rearrange("(k p) n -> p k n", p=128))],
                      reads=[pmb_], writes=[wkb[1], wkb[2]])
                if NLOC != NSEQ:
                    o2 = NLOC + o
                    P.dma("sp", zsem3b, lambda e, o2=o2: [e.dma_start(out=V(work, 12288, [[512, 8], [1, 512]]),
                                                                     in_=z_d[:, o2:o2 + T].rearrange("(k p) n -> p k n", p=128))],
                          reads=[zb_], writes=[wkb[3]])
                    P.dma("sp", pmsem3b, lambda e, o2=o2: [e.dma_start(out=V(work, 16384, [[512, 16], [1, 512]]),
                                                                      in_=pm_d[:, o2:o2 + T].rearrange("(k p) n -> p k n", p=128))],
                          reads=[pmb_], writes=[wkb[4], wkb[5]])
                    P.op("dve", lambda e: e.tensor_scalar(out=work[:, 12288:24576], in0=work[:, 12288:24576], scalar1=m1, scalar2=None,
                                                          op0=ALU.mult), reads=[wkb[3], wkb[4], wkb[5], cb], writes=[wkb[3], wkb[4], wkb[5]])
                    P.op("dve", lambda e: e.scalar_tensor_tensor(out=work[:, 0:12288], in0=work[:, 0:12288], scalar=m0,
                                                                 in1=work[:, 12288:24576], op0=ALU.mult, op1=ALU.add),
                         reads=[wkb[i] for i in range(6)] + [cb], writes=[wkb[0], wkb[1], wkb[2]])

                def zr(k):
                    return work[:, k * 512:(k + 1) * 512], wkb[0]

                def glu_slot(c):
                    return work[:, 16384 + c * 512:16384 + (c + 1) * 512], wkb[4]

                def pw_slot(c):
                    return work[:, 20480 + c * 512:20480 + (c + 1) * 512], wkb[5 + c // 8]

                def mg_slot(c):
                    return work[:, c * 512:(c + 1) * 512], wkb[c // 8]
                for c in range(8):
                    ba, bb_ = bank(), bank()
                    mm_group(S, wglu_d, 0, 8, c * 128, 128, zr, T, [ba])
                    mm_group(S, wglu_d, 0, 8, 1024 + c * 128, 128, zr, T, [bb_])
                    i = S["rr"][1] % 2
                    S["rr"][1] += 1
                    sg = S["sg"][i]
                    gap, gbf = glu_slot(c)
                    P.op("act", lambda e, sg=sg, bb_=bb_: e.activation(out=sg[:, :T], in_=bb_.t[:, :T], func=AF.Sigmoid),
                         reads=[bb_.buf], writes=[S["sgb"][i]])
                    P.op("dve", lambda e, sg=sg, ba=ba, gap=gap: e.tensor_tensor(out=gap, in0=sg[:, :T], in1=ba.t[:, :T], op=ALU.mult),
                         reads=[S["sgb"][i], ba.buf], writes=[gbf])
                for g in range(4):
                    for cp in range(2):
                        bks = [bank(), bank()]
                        mm_group(S, pw_d, g * 512, 4, cp * 256, 256,
                                 lambda k, g=g: (work[:, 4096 + (g * 4 + k) * 512:4096 + (g * 4 + k + 1) * 512], wkb[1 + (g * 4 + k) // 8]), T, bks)
                        for cc in range(2):
                            ch = g * 4 + cp * 2 + cc
                            pap, pbf = pw_slot(ch)
                            P.op("act", lambda e, pap=pap, b=bks[cc], ch=ch: e.activation(out=pap, in_=b.t[:, :T], func=AF.Identity,
                                                                                         scale=pscT[:, ch:ch + 1]),
                                 reads=[bks[cc].buf, cb], writes=[pbf])
                for dp in range(16):
                    mtmp = []
                    for (Wg, gcol, Wy, nky, rfn) in ((win_d, 3072, wba_d, 8, glu_slot), (win_d, 7168, wbb_d, 16, pw_slot)):
                        bg = [bank(), bank()]
                        mm_group(S, Wg, 0, 32, gcol + dp * 256, 256, lambda k: (xn[:, k, :T], S["xnb"][k]), T, bg)
                        by = [bank(), bank()]
                        mm_group(S, Wy, 0, nky, dp * 256, 256, rfn, T, by)
                        row = []
                        for cc in range(2):
                            i = S["rr"][1] % 2
                            S["rr"][1] += 1
                            sg = S["sg"][i]
                            j = S["rr"][0] % 3
                            S["rr"][0] += 1
                            o32 = S["o32"][j]
                            P.op("act", lambda e, sg=sg, b=bg[cc]: e.activation(out=sg[:, :T], in_=b.t[:, :T], func=AF.Sigmoid),
                                 reads=[bg[cc].buf], writes=[S["sgb"][i]])
                            if not mtmp:
                                P.op("dve", lambda e, sg=sg, b=by[cc], o32=o32: e.tensor_tensor(out=o32[:, :T], in0=sg[:, :T], in1=b.t[:, :T], op=ALU.mult),
                                     reads=[S["sgb"][i], by[cc].buf], writes=[S["o32b"][j]])
                                row.append((o32, S["o32b"][j]))
                            else:
                                po, pob = mtmp[0][cc]
                                map_, mbf = mg_slot(dp * 2 + cc)
                                P.op("dve", lambda e, sg=sg, b=by[cc]: e.tensor_tensor(out=sg[:, :T], in0=sg[:, :T], in1=b.t[:, :T], op=ALU.mult),
                                     reads=[S["sgb"][i], by[cc].buf], writes=[S["sgb"][i]])
                                P.op("pool", lambda e, sg=sg, po=po, map_=map_: e.tensor_tensor(out=map_, in0=sg[:, :T], in1=po[:, :T], op=ALU.add),
                                     reads=[S["sgb"][i], pob], writes=[mbf])
                        mtmp.append(row)
                for dp in range(16):
                    bks = [bank(), bank()]
                    mm_group(S, wout_d, 0, 32, dp * 256, 256, mg_slot, T, bks)
                    for cc in range(2):
                        evac_add(S, bks[cc], dp * 2 + cc, mv(0, 2, 1), T)
                norm_to_xn(S, ng, mv(0, 0, 2), mv(0, 1, 2))
                ffn(S, f2i_d, f2o_d, mv(0, 2, 2), T)
                fgv = work[:, 0:8192].bitcast(F32)
                P.dma("sp", fgsem, lambda e: [e.dma_start(out=fgv, in_=fg_d.partition_broadcast(128))], writes=[wkb[0], wkb[1]])
                for g in range(ng):
                    junk = work[:, 8192:12288]
                    rstd_of(S, g, V(work, 8192, [[128, 32], [1, 128]]), [wkb[2]])
                    P.op("dve", lambda e, g=g: e.scalar_tensor_tensor(out=S["xt"][:, g, :], in0=S["xt"][:, g, :], scalar=S["st"][:, g:g + 1],
                                                                     in1=fgv, op0=ALU.mult, op1=ALU.mult),
                         reads=[S["xtb"][g], S["stb"][g], wkb[0], wkb[1]], writes=[S["xtb"][g]])
                store_x(S, out_d[o:o + T, :], ng)
        P.emit()
    return nc


def _fm(v, n):
    return np.ascontiguousarray(np.asarray(v, np.float32).reshape(n, 128).T)


def prep_shared(inp):
    sh = {}
    sh["w_mod"] = np.ascontiguousarray(inp["w_mod"][0])
    sh["b_modT"] = _fm(inp["b_mod"][0], 288)
    sh["norm_gT"] = np.ascontiguousarray(np.concatenate([_fm(inp["norm_g"][0, i], 32) for i in range(3)], axis=1))
    sh["final_g"] = np.ascontiguousarray(np.asarray(inp["final_g"], np.float32).reshape(1, D))
    for k in ("ffn1_w_in", "ffn1_w_out", "ffn2_w_in", "ffn2_w_out", "w_in", "w_glu", "w_branch_a", "w_branch_b", "w_out"):
        sh[k] = np.ascontiguousarray(inp[k][0])
    sh["pool_w"] = np.ascontiguousarray(inp["pool_w"][0].reshape(2048, 512))
    sh["pool_scaleT"] = _fm(inp["pool_scale"][0], 16)
    sh["ssm_dT"] = _fm(inp["ssm_d"][0], 8)
    lre, lim, ls = inp["ssm_lambda_re"][0], inp["ssm_lambda_im"][0], inp["ssm_log_step"][0]
    bre, bim = inp["ssm_b_re"][0], inp["ssm_b_im"][0]
    cre, cim = inp["ssm_c_re"][0], inp["ssm_c_im"][0]
    ls3 = np.broadcast_to(ls[:, :, None], (2, 64, 64))

    def st_layout(a):
        a = np.asarray(a, np.float32).reshape(2, 8, 8, 4, 16)
        return a.transpose(2, 4, 0, 1, 3).reshape(128, 2, 32)
    sp = np.stack([st_layout(lre), st_layout(lim), st_layout(ls3)], axis=2)
    sh["ssm_sp"] = np.ascontiguousarray(sp.reshape(128, 192))

    def b_layout(a):
        a = np.asarray(a, np.float32).reshape(2, 8, 8, 4, 16, 16)
        return a.transpose(2, 5, 0, 1, 3, 4).reshape(128, 2, 512)

    def rep_k(a):
        return np.broadcast_to(np.asarray(a, np.float32)[..., None], (2, 64, 64, 16))
    bp = np.stack([b_layout(rep_k(lre)), b_layout(rep_k(lim)), b_layout(rep_k(ls3)), b_layout(bre), b_layout(bim)], axis=2)
    sh["ssm_bp"] = np.ascontiguousarray(bp.reshape(128, 5120))

    def c_layout(a):
        a = np.asarray(a, np.float32).reshape(2, 8, 8, 16, 4, 16)
        return a.transpose(2, 5, 0, 1, 4, 3).reshape(128, 2, 512)
    cpp = np.stack([c_layout(cre), c_layout(cim)], axis=2)
    sh["ssm_cp"] = np.ascontiguousarray(cpp.reshape(128, 2048))

    def s_layout_b(a):
        a = np.asarray(a, np.float32).reshape(2, 8, 8, 4, 16, 16)
        return a.transpose(2, 4, 0, 1, 3, 5).reshape(128, 2, 512)
    bsp = np.stack([s_layout_b(bre), s_layout_b(bim)], axis=2)
    sh["ssm_bs"] = np.ascontiguousarray(bsp.reshape(128, 2048))
    bm = np.zeros((128, 8), np.float32)
    bm[np.arange(128), np.arange(128) // 16] = 1.0
    sh["bmask"] = bm
    sh["ident"] = np.eye(128, dtype=np.float32)
    sh["iota"] = np.ascontiguousarray(np.broadcast_to(np.arange(513, dtype=np.float32), (128, 513)))
    return sh


def core_inputs(inp, sh, b, half):
    m = dict(sh)
    m["x"] = np.ascontiguousarray(inp["x"][b])
    m["ctx"] = np.ascontiguousarray(inp["ctx"][b])
    c2 = np.stack([_fm(inp["c"][b], 32), _fm(inp["c_ctx"], 32)], axis=2)
    m["c2"] = np.ascontiguousarray(c2.reshape(128, 64))
    hs = np.zeros((128, 2), np.float32)
    hs[:, half] = 1.0
    m["hsel"] = hs
    return m


def kernel(**inputs):
    inp = {k: np.asarray(v) for k, v in inputs.items()}
    B, NSEQ = inp["x"].shape[0], inp["x"].shape[1]
    NLOC = NSEQ // 2
    nc = build(NLOC, NSEQ)
    sh = prep_shared(inp)
    in_maps = [core_inputs(inp, sh, c // 2, c % 2) for c in range(8)]
    res = run_bass_kernel_spmd(nc, in_maps, core_ids=list(range(8)))
    out = np.empty((B, NSEQ, D), np.float32)
    for c in range(8):
        out[c // 2, (c % 2) * NLOC:(c % 2 + 1) * NLOC] = res.results[c]["out"]
    return out
```
